# Optimizing a Trainium2 kernel written in Bass

```python
import jax, jax.numpy as jnp
from jax import lax
import numpy as np

D_MODEL = 2048
BATCH = 4
SEQ = 8192
DEPTH = 1

D_MIX = D_MODEL
GM_WIDTH = D_MIX // 2
LRU_WIDTH = D_MIX - GM_WIDTH
CHUNK = 128
GM_HEADS = 8
GM_HEAD_DIM = GM_WIDTH // GM_HEADS
LRU_HEADS = 8
LRU_BLOCK = LRU_WIDTH // LRU_HEADS
LRU_CONV = 4
LRU_C = 8.0
FFN_MULT = 3
D_FF = FFN_MULT * D_MODEL
FFN_CONV = 3
IN_COLS = 2 * GM_WIDTH + 2 * LRU_WIDTH
RMS_EPS = 1e-6
LN_EPS = 1e-5

kernel_name = "hymba_style_gmlp_rglru_convffn"


def _rmsnorm(x, g):
    xf = x.astype(jnp.float32)
    y = xf * lax.rsqrt(jnp.mean(xf * xf, axis=-1, keepdims=True) + RMS_EPS)
    return (y * g.astype(jnp.float32)).astype(x.dtype)


def _layernorm(x, g, b):
    xf = x.astype(jnp.float32)
    mu = jnp.mean(xf, axis=-1, keepdims=True)
    xc = xf - mu
    y = xc * lax.rsqrt(jnp.mean(xc * xc, axis=-1, keepdims=True) + LN_EPS)
    return (y * g.astype(jnp.float32) + b.astype(jnp.float32)).astype(x.dtype)


def _causal_dwconv(x, w, b):
    k_width = w.shape[0]
    s = x.shape[1]
    xp = jnp.pad(x, ((0, 0), (k_width - 1, 0), (0, 0)))
    y = b
    for k in range(k_width):
        y = y + xp[:, k:k + s] * w[k]
    return y


def _spatial_gating(z, v_g, v_b, ws, bs):
    u, v = jnp.split(z, 2, axis=-1)
    v = _layernorm(v, v_g, v_b)
    bsz, s, _ = v.shape
    vc = v.reshape(bsz, s // CHUNK, CHUNK, GM_HEADS, GM_HEAD_DIM)
    mask = jnp.tril(jnp.ones((CHUNK, CHUNK), dtype=bool))
    w = jnp.where(mask[None], ws, jnp.zeros((), ws.dtype))
    mixed = jnp.einsum('hts,bcshd->bcthd', w, vc) + bs.T[None, None, :, :, None]
    return u * mixed.reshape(bsz, s, GM_WIDTH)


def _lru_combine(left, right):
    a1, b1 = left
    a2, b2 = right
    return a1 * a2, a2 * b1 + b2


def _rg_lru(x, wa, ba, wx, bx, lam):
    bsz, s, w = x.shape
    xh = x.reshape(bsz, s, LRU_HEADS, LRU_BLOCK)
    r = jax.nn.sigmoid(jnp.einsum('bshi,hij->bshj', xh, wa) + ba).reshape(bsz, s, w)
    i = jax.nn.sigmoid(jnp.einsum('bshi,hij->bshj', xh, wx) + bx).reshape(bsz, s, w)
    log_a = -LRU_C * r.astype(jnp.float32) * jax.nn.softplus(-lam.astype(jnp.float32))
    a = jnp.exp(log_a)
    mult = jnp.sqrt(-jnp.expm1(2.0 * log_a))
    b = mult * (i * x).astype(jnp.float32)
    _, h = lax.associative_scan(_lru_combine, (a, b), axis=1)
    return h.astype(x.dtype)


def setup_inputs(seed: int = 0) -> dict:
    key = jax.random.key(seed)
    ks = jax.random.split(key, 24)
    f32 = jnp.float32
    L = DEPTH

    def nrm(k, shape, scale):
        return jax.random.normal(k, shape, f32) * scale

    x = jax.random.normal(ks[0], (BATCH, SEQ, D_MODEL), f32)
    norm1_g = 1.0 + nrm(ks[1], (L, D_MODEL), 0.05)
    w_in = nrm(ks[2], (L, D_MODEL, IN_COLS), D_MODEL ** -0.5)
    gm_v_g = 1.0 + nrm(ks[3], (L, GM_WIDTH), 0.05)
    gm_v_b = nrm(ks[4], (L, GM_WIDTH), 0.02)
    gm_ws = nrm(ks[5], (L, GM_HEADS, CHUNK, CHUNK), CHUNK ** -0.5)
    gm_bs = 1.0 + nrm(ks[6], (L, GM_HEADS, CHUNK), 0.1)
    lru_conv_w = nrm(ks[7], (L, LRU_CONV, LRU_WIDTH), LRU_CONV ** -0.5)
    lru_conv_b = nrm(ks[8], (L, LRU_WIDTH), 0.02)
    lru_wa = nrm(ks[9], (L, LRU_HEADS, LRU_BLOCK, LRU_BLOCK), LRU_BLOCK ** -0.5)
    lru_ba = nrm(ks[10], (L, LRU_HEADS, LRU_BLOCK), 0.02)
    lru_wx = nrm(ks[11], (L, LRU_HEADS, LRU_BLOCK, LRU_BLOCK), LRU_BLOCK ** -0.5)
    lru_bx = nrm(ks[12], (L, LRU_HEADS, LRU_BLOCK), 0.02)
    a_c = jax.random.uniform(ks[13], (L, LRU_WIDTH), f32, 0.9, 0.999)
    a_base = a_c ** (1.0 / LRU_C)
    lru_lambda = jnp.log(a_base) - jnp.log1p(-a_base)
    gm_out_g = 1.0 + nrm(ks[14], (L, GM_WIDTH), 0.05)
    lru_out_g = 1.0 + nrm(ks[15], (L, LRU_WIDTH), 0.05)
    w_out = nrm(ks[16], (L, D_MIX, D_MODEL), D_MIX ** -0.5)
    norm2_g = 1.0 + nrm(ks[17], (L, D_MODEL), 0.05)
    ffn_w_up = nrm(ks[18], (L, D_MODEL, 2 * D_FF), D_MODEL ** -0.5)
    ffn_conv_w = nrm(ks[19], (L, FFN_CONV, 2 * D_FF), FFN_CONV ** -0.5)
    ffn_conv_b = nrm(ks[20], (L, 2 * D_FF), 0.02)
    ffn_w_down = nrm(ks[21], (L, D_FF, D_MODEL), D_FF ** -0.5)
    final_g = 1.0 + nrm(ks[22], (D_MODEL,), 0.05)
    return {
        "x": x, "norm1_g": norm1_g, "w_in": w_in,
        "gm_v_g": gm_v_g, "gm_v_b": gm_v_b, "gm_ws": gm_ws, "gm_bs": gm_bs,
        "lru_conv_w": lru_conv_w, "lru_conv_b": lru_conv_b,
        "lru_wa": lru_wa, "lru_ba": lru_ba, "lru_wx": lru_wx, "lru_bx": lru_bx,
        "lru_lambda": lru_lambda, "gm_out_g": gm_out_g, "lru_out_g": lru_out_g,
        "w_out": w_out, "norm2_g": norm2_g, "ffn_w_up": ffn_w_up,
        "ffn_conv_w": ffn_conv_w, "ffn_conv_b": ffn_conv_b, "ffn_w_down": ffn_w_down,
        "final_g": final_g,
    }


def reference(x, norm1_g, w_in, gm_v_g, gm_v_b, gm_ws, gm_bs, lru_conv_w, lru_conv_b,
              lru_wa, lru_ba, lru_wx, lru_bx, lru_lambda, gm_out_g, lru_out_g, w_out,
              norm2_g, ffn_w_up, ffn_conv_w, ffn_conv_b, ffn_w_down, final_g):
    for l in range(DEPTH):
        h = _rmsnorm(x, norm1_g[l])
        p = jnp.einsum('bsd,de->bse', h, w_in[l])
        z_gm = p[..., :2 * GM_WIDTH]
        g_lru = p[..., 2 * GM_WIDTH:2 * GM_WIDTH + LRU_WIDTH]
        x_lru = p[..., 2 * GM_WIDTH + LRU_WIDTH:]
        y_gm = _spatial_gating(jax.nn.gelu(z_gm), gm_v_g[l], gm_v_b[l], gm_ws[l], gm_bs[l])
        xr = _causal_dwconv(x_lru, lru_conv_w[l], lru_conv_b[l])
        y_lru = _rg_lru(xr, lru_wa[l], lru_ba[l], lru_wx[l], lru_bx[l], lru_lambda[l])
        y_lru = y_lru * jax.nn.gelu(g_lru)
        y = jnp.concatenate([_rmsnorm(y_gm, gm_out_g[l]), _rmsnorm(y_lru, lru_out_g[l])], axis=-1)
        x = x + jnp.einsum('bse,ed->bsd', y, w_out[l])
        h = _rmsnorm(x, norm2_g[l])
        up = jnp.einsum('bsd,df->bsf', h, ffn_w_up[l])
        up = _causal_dwconv(up, ffn_conv_w[l], ffn_conv_b[l])
        gate, val = jnp.split(up, 2, axis=-1)
        x = x + jnp.einsum('bsf,fd->bsd', jax.nn.gelu(gate) * val, ffn_w_down[l])
    return _rmsnorm(x, final_g)
```

```python
import numpy as np
from contextlib import ExitStack
import concourse.bass as bass
import concourse.mybir as mybir
from concourse.bass_utils import run_bass_kernel_spmd

F32 = mybir.dt.float32
BF16 = mybir.dt.bfloat16
AF = mybir.ActivationFunctionType
ALU = mybir.AluOpType

D = 2048
TT = 512
NSUB = 4
NH = 8
DFF = 6144
NW = 2
N_CORES = 8
SEQ = 8192
BATCH = 4


class Tile:
    __slots__ = ("name", "lw", "rd")

    def __init__(self, name):
        self.name = name
        self.lw = None
        self.rd = {}


class Sched:
    ENG = ("pe", "act", "dve", "pool", "sp")

    def __init__(self, nc, es):
        self.nc = nc
        self.es = es
        self.ops = {e: [] for e in self.ENG}
        self.cnt = {e: 0 for e in self.ENG}
        self.known = {e: {} for e in self.ENG}
        self.semobj = {e: es.enter_context(nc.semaphore("s_" + e)) for e in self.ENG}
        self.dcnt = {}

    def dsem(self, key):
        self.semobj[key] = self.es.enter_context(self.nc.semaphore("d_" + key))
        self.dcnt[key] = 0
        return key

    def _need(self, e, dep, waits):
        key, val = dep
        if self.known[e].get(key, 0) >= val:
            return
        if waits.get(key, 0) < val:
            waits[key] = val

    def _emit_waits(self, e, waits):
        for k, v in waits.items():
            self.known[e][k] = v
            self.ops[e].append(("wait", k, v))

    def op(self, e, fn, reads=(), writes=(), inc=True):
        waits = {}
        for t in reads:
            if t.lw is not None:
                if t.lw[0] == e and e == "pe":
                    continue
                self._need(e, t.lw, waits)
        for t in writes:
            if t.lw is not None and t.lw[0] != e:
                self._need(e, t.lw, waits)
            for k, v in t.rd.items():
                if k != e:
                    self._need(e, (k, v), waits)
        self._emit_waits(e, waits)
        newc = self.cnt[e] + 1
        self.ops[e].append(("op", fn, inc))
        if inc:
            self.cnt[e] = newc
        for t in reads:
            if t.rd.get(e, 0) < newc:
                t.rd[e] = newc
        for t in writes:
            t.lw = (e, newc)
            t.rd = {}

    def dma(self, q, fn, dkey, reads=(), writes=()):
        waits = {}
        for t in reads:
            if t.lw is not None:
                self._need(q, t.lw, waits)
        for t in writes:
            if t.lw is not None:
                self._need(q, t.lw, waits)
            for k, v in t.rd.items():
                self._need(q, (k, v), waits)
        self._emit_waits(q, waits)
        newv = self.dcnt[dkey] + 16
        self.dcnt[dkey] = newv
        self.ops[q].append(("dma", fn, dkey))
        for t in reads:
            t.rd[dkey] = newv
        for t in writes:
            t.lw = (dkey, newv)
            t.rd = {}

    def wait_all(self, e, keys):
        waits = {}
        for k in keys:
            v = self.dcnt[k] if k in self.dcnt else self.cnt[k]
            if v > 0:
                self._need(e, (k, v), waits)
        self._emit_waits(e, waits)

    def emit(self, e, eng):
        for o in self.ops[e]:
            if o[0] == "wait":
                eng.wait_ge(self.semobj[o[1]], o[2])
            elif o[0] == "op":
                ins = o[1](eng)
                if o[2]:
                    ins.then_inc(self.semobj[e], 1)
            else:
                o[1](eng).then_inc(self.semobj[o[2]], 16)


def build(nmain, debug=False):
    npre = nmain - 1
    ntiles = 2 * nmain
    nc = bass.Bass("TRN2", target_bir_lowering=False)
    es = ExitStack()

    def din(name, shape):
        return nc.dram_tensor(name, list(shape), F32, kind="ExternalInput").ap()

    xin = din("xin", [ntiles * TT, D])
    flag_d = din("flag", [128, 1])
    norm1_g = din("norm1_g", [1, D])
    w_in = din("w_in", [1, D, 4096])
    gm_v_g = din("gm_v_g", [1, 1024])
    gm_v_b = din("gm_v_b", [1, 1024])
    gm_ws = din("gm_ws", [1, 8, 128, 128])
    gm_bs = din("gm_bs", [1, 8, 128])
    lru_conv_w = din("lru_conv_w", [1, 4, 1024])
    lru_conv_b = din("lru_conv_b", [1, 1024])
    lru_wa = din("lru_wa", [1, 8, 128, 128])
    lru_ba = din("lru_ba", [1, 8, 128])
    lru_wx = din("lru_wx", [1, 8, 128, 128])
    lru_bx = din("lru_bx", [1, 8, 128])
    lru_lambda = din("lru_lambda", [1, 1024])
    gm_out_g = din("gm_out_g", [1, 1024])
    lru_out_g = din("lru_out_g", [1, 1024])
    w_out = din("w_out", [1, D, D])
    norm2_g = din("norm2_g", [1, D])
    ffn_w_up = din("ffn_w_up", [1, D, 2 * DFF])
    ffn_conv_w = din("ffn_conv_w", [1, 3, 2 * DFF])
    ffn_conv_b = din("ffn_conv_b", [1, 2 * DFF])
    ffn_w_down = din("ffn_w_down", [1, DFF, D])
    final_g = din("final_g", [D])
    out = nc.dram_tensor("out", [nmain * TT, D], F32, kind="ExternalOutput").ap()

    if debug:
        dbg_xmid = nc.dram_tensor("dbg_xmid", [TT, D], F32, kind="ExternalOutput").ap()
        dbg_yT = nc.dram_tensor("dbg_yT", [128, 16, TT], BF16, kind="ExternalOutput").ap()
        dbg_hT = nc.dram_tensor("dbg_hT", [128, 16, TT], BF16, kind="ExternalOutput").ap()
        dbg_hid = nc.dram_tensor("dbg_hid", [128, 16, TT], BF16, kind="ExternalOutput").ap()
        dbg_xffn = nc.dram_tensor("dbg_xffn", [TT, D], F32, kind="ExternalOutput").ap()
    s_in = nc.dram_tensor("s_in", [8, 128, 16, 512], BF16, kind="Internal").ap()
    s_out = nc.dram_tensor("s_out", [4, 128, 16, 512], BF16, kind="Internal").ap()
    s_up = nc.dram_tensor("s_up", [24, 128, 16, 512], BF16, kind="Internal").ap()
    s_dn = nc.dram_tensor("s_dn", [12, 128, 16, 512], BF16, kind="Internal").ap()

    with es:
        S = Sched(nc, es)

        def sb(name, shape, dt):
            return es.enter_context(nc.sbuf_tensor(name, list(shape), dt))

        xres = sb("xres", [128, NSUB, D], F32)
        hT = sb("hT", [128, 16, TT], BF16)
        yT = sb("yT", [128, 16, TT], BF16)
        wr = [sb("wr%d" % i, [128, 16, 512], BF16) for i in range(NW)]
        ident = sb("ident", [128, 128], F32)
        ones_f = sb("ones_f", [128, 128], F32)
        ones_b = sb("ones_b", [128, 2], BF16)
        epsc = sb("epsc", [128, 2], F32)
        cv = sb("cv", [128, 512], F32)
        dc = sb("dc", [128, 24], F32)
        wab = sb("wab", [128, 8, 128], BF16)
        wxb = sb("wxb", [128, 8, 128], BF16)
        wmtb = sb("wmtb", [128, 8, 128], BF16)
        bh = sb("bh", [128, 8, 128], F32)
        gfb = sb("gfb", [128, D], F32)
        flag = sb("flag_s", [128, 1], F32)
        rb4 = sb("rb4", [128, 4, 128], F32)
        rts = sb("rts", [128, TT], F32)
        sm = sb("sm", [128, 64], F32)
        state = sb("state", [128, 8], F32)
        uh = sb("uh", [128, 96, 2], F32)
        xl = sb("xl", [128, 8, 516], F32)
        xr = sb("xr", [128, 2, TT], F32)
        xrb = sb("xrb", [128, 2, TT], BF16)
        tmp = sb("tmp", [128, 8, 516], F32)
        gf = sb("gf", [128, 8, TT], F32)
        vg = sb("vg", [128, 1024], F32)
        vn = sb("vn", [128, NSUB, 1024], BF16)
        ug = sb("ug", [128, 2, TT], F32)
        t1 = sb("t1", [128, 2, TT], F32)
        ysq = sb("ysq", [128, 8, TT], BF16)
        ps = [es.enter_context(nc.psum_tensor("ps%d" % i, [128, 512], F32)) for i in range(8)]

        T = {}

        def tk(name):
            if name not in T:
                T[name] = Tile(name)
            return T[name]

        PS = [tk("ps%d" % i) for i in range(8)]
        bank = [0]

        def nb():
            b = bank[0]
            bank[0] = (b + 1) % 8
            return b

        junk = vg[:].bitcast(BF16)
        hid0 = gf[:].rearrange("p a b -> p (a b)").bitcast(BF16)

        def hid_ap(buf, k):
            if buf == 0:
                return hid0[:, k * TT:(k + 1) * TT]
            return yT[:, k, :]

        def hid_tk(buf, k):
            if buf == 0:
                return tk("gf%d" % (k // 2))
            return tk("yT%d" % k)

        D_STG = S.dsem("stg")
        cst_sems = [D_STG]
        D_CVT = {g_: S.dsem("cvt" + g_) for g_ in "ABCD"}
        D_W = [S.dsem("w%d" % i) for i in range(NW)]
        D_X = [S.dsem("x%d" % j) for j in range(NSUB)]
        D_O = [S.dsem("o%d" % j) for j in range(NSUB)]

        win_v = w_in[0].rearrange("(k p) (c j) -> c p k j", p=128, j=512)
        wout_v = w_out[0].rearrange("(k p) (c j) -> c p k j", p=128, j=512)
        wup_v = ffn_w_up[0].rearrange("(k p) (c j) -> c p k j", p=128, j=512)
        wdn_v = ffn_w_down[0].rearrange("(g k p) (c j) -> g c p k j", k=16, p=128, j=512)
        cvt_jobs = []
        for c in (6, 7):
            cvt_jobs.append(("A", s_in[c], win_v[c], "s_in%d" % c))
        for c in (4, 5, 2, 3, 0, 1):
            cvt_jobs.append(("B", s_in[c], win_v[c], "s_in%d" % c))
        for c in range(4):
            cvt_jobs.append(("B", s_out[c], wout_v[c], "s_out%d" % c))
        for c in range(24):
            cvt_jobs.append(("C", s_up[c], wup_v[c], "s_up%d" % c))
        for g in range(3):
            for c in range(4):
                cvt_jobs.append(("D", s_dn[g * 4 + c], wdn_v[g, c], "s_dn%d" % (g * 4 + c)))
        cvt_left = {g_: sum(1 for j_ in cvt_jobs if j_[0] == g_) for g_ in "ABCD"}
        cvt_tiles = {g_: [] for g_ in "ABCD"}
        cvt_pos = [0]

        def emit_cvts(n):
            while n > 0 and cvt_pos[0] < len(cvt_jobs):
                g_, dst, src, name = cvt_jobs[cvt_pos[0]]
                cvt_pos[0] += 1
                n -= 1
                t = tk(name)
                cvt_tiles[g_].append(t)
                S.dma("pool", lambda e, dst=dst, src=src: e.dma_start(out=dst, in_=src), D_CVT[g_], writes=[t])
                cvt_left[g_] -= 1
                if cvt_left[g_] == 0:
                    for t2 in cvt_tiles[g_]:
                        t2.lw = (D_CVT[g_], S.dcnt[D_CVT[g_]])

        emit_cvts(2)

        S.op("pool", lambda e: e.memset(ident[:], 0.0), writes=[tk("ident")])
        S.op("pool", lambda e: e.affine_select(out=ident[:], in_=ident[:], pattern=[[-1, 128]],
                                               compare_op=ALU.not_equal, fill=1.0, base=0,
                                               channel_multiplier=1),
             reads=[tk("ident")], writes=[tk("ident")])
        S.op("pool", lambda e: e.memset(ones_f[:], 1.0), writes=[tk("ones_f")])
        S.op("pool", lambda e: e.memset(ones_b[:], 1.0), writes=[tk("ones_b")])
        S.op("pool", lambda e: e.memset(epsc[:, 0:1], 1e-6), writes=[tk("epsc")])
        S.op("pool", lambda e: e.memset(epsc[:, 1:2], 1e-5), writes=[tk("epsc")])
        S.op("pool", lambda e: e.memset(state[:], 0.0), writes=[tk("state")])
        S.op("pool", lambda e: e.memset(uh[:], 0.0), writes=[tk("uh")])
        S.op("pool", lambda e: e.memset(xl[:], 0.0), writes=[tk("xl%d" % h) for h in range(8)])

        def cdma(dst, src, wt, q="sp", key=None):
            if key is None:
                key = S.dsem("c%d" % len(cst_sems))
                cst_sems.append(key)
            S.dma(q, lambda e, dst=dst, src=src: e.dma_start(out=dst, in_=src), key, writes=wt)

        stage = tmp[:, 0, 0:128]
        T_stage = tk("tmp0")
        rows_a = [
            (norm1_g.rearrange("o (k p) -> (o k) p", p=128), 16),
            (norm2_g.rearrange("o (k p) -> (o k) p", p=128), 16),
            (gm_v_g.rearrange("o (k p) -> (o k) p", p=128), 8),
            (gm_v_b.rearrange("o (k p) -> (o k) p", p=128), 8),
            (lru_conv_w.rearrange("o c (k p) -> (o c k) p", p=128), 32),
            (lru_conv_b.rearrange("o (k p) -> (o k) p", p=128), 8),
            (lru_ba.rearrange("o h p -> (o h) p"), 8),
            (lru_bx.rearrange("o h p -> (o h) p"), 8),
            (lru_lambda.rearrange("o (k p) -> (o k) p", p=128), 8),
            (gm_out_g.rearrange("o (k p) -> (o k) p", p=128), 8),
            (lru_out_g.rearrange("o (k p) -> (o k) p", p=128), 8),
        ]
        fcw_rows = ffn_conv_w.rearrange("o c (k p) -> (o c k) p", p=128)
        fcb_rows = ffn_conv_b.rearrange("o (k p) -> (o k) p", p=128)
        groups = [rows_a,
                  [(fcw_rows[0:128], 128)],
                  [(fcw_rows[128:256], 128)],
                  [(fcw_rows[256:288], 32), (fcb_rows, 96)]]
        for gi, grp in enumerate(groups):
            r0 = 0
            for src, n in grp:
                cdma(stage[r0:r0 + n, :], src, [T_stage], key=D_STG)
                r0 += n
            b = nb()
            S.op("pe", lambda e, b=b: e.transpose(out=ps[b][:, 0:128], in_=stage, identity=ident[:]),
                 reads=[T_stage, tk("ident")], writes=[PS[b]])
            S.op("dve", lambda e, b=b, gi=gi: e.tensor_copy(out=cv[:, gi * 128:(gi + 1) * 128], in_=ps[b][:, 0:128]),
                 reads=[PS[b]], writes=[tk("cv")])
        C_G1, C_G2, C_GVG, C_GVB, C_LCW, C_LCB, C_LBA, C_LBX, C_LAM, C_GMO, C_LRO = 0, 16, 32, 40, 48, 80, 88, 96, 104, 112, 120
        C_FCW, C_FCB = 128, 416

        def col(c):
            return cv[:, c:c + 1]

        S.op("dve", lambda e: e.tensor_scalar_mul(out=dc[:, 0:8], in0=cv[:, C_LBA:C_LBA + 8], scalar1=0.5),
             reads=[tk("cv")], writes=[tk("dc")])
        S.op("dve", lambda e: e.tensor_scalar_mul(out=dc[:, 8:16], in0=cv[:, C_LBX:C_LBX + 8], scalar1=0.5),
             reads=[tk("cv")], writes=[tk("dc")])
        T_sm = tk("sm")
        S.op("act", lambda e: e.activation(out=sm[:, 0:8], in_=cv[:, C_LAM:C_LAM + 8], func=AF.Exp, scale=-1.0),
             reads=[tk("cv")], writes=[T_sm])
        S.op("dve", lambda e: e.tensor_scalar_add(out=sm[:, 8:16], in0=sm[:, 0:8], scalar1=1.0),
             reads=[T_sm], writes=[T_sm])
        S.op("act", lambda e: e.activation(out=sm[:, 16:24], in_=sm[:, 8:16], func=AF.Ln),
             reads=[T_sm], writes=[T_sm])
        S.op("dve", lambda e: e.tensor_scalar(out=sm[:, 24:32], in0=sm[:, 8:16], scalar1=-1.0, scalar2=1e-30,
                                              op0=ALU.add, op1=ALU.max),
             reads=[T_sm], writes=[T_sm])
        S.op("dve", lambda e: e.reciprocal(out=sm[:, 24:32], in_=sm[:, 24:32]), reads=[T_sm], writes=[T_sm])
        S.op("dve", lambda e: e.tensor_mul(out=sm[:, 16:24], in0=sm[:, 16:24], in1=sm[:, 0:8]),
             reads=[T_sm], writes=[T_sm])
        S.op("dve", lambda e: e.tensor_mul(out=sm[:, 16:24], in0=sm[:, 16:24], in1=sm[:, 24:32]),
             reads=[T_sm], writes=[T_sm])
        S.op("dve", lambda e: e.tensor_scalar_mul(out=dc[:, 16:24], in0=sm[:, 16:24], scalar1=2.0),
             reads=[T_sm], writes=[tk("dc")])

        cdma(wab[:], lru_wa[0].rearrange("h i j -> i h j"), [tk("wab")], q="pool")
        cdma(wxb[:], lru_wx[0].rearrange("h i j -> i h j"), [tk("wxb")], q="pool")
        cdma(gfb[:], final_g.partition_broadcast(128), [tk("gfb")])
        cdma(flag[:], flag_d, [tk("flag")])

        wsf = vg[:].rearrange("p (h s) -> p h s", h=8)
        T_vg = tk("vg")
        cdma(wsf, gm_ws[0].rearrange("h t s -> t h s"), [T_vg])
        bsb = ug[:].rearrange("p a b -> p (a b)")
        T_ug = [tk("ug0"), tk("ug1")]
        cdma(bsb, gm_bs.rearrange("o h t -> (o h t)").partition_broadcast(128), T_ug)
        wmtf = t1[:].rearrange("p a b -> p (a b)")
        T_t1 = [tk("t10"), tk("t11")]
        for h in range(8):
            b = nb()
            S.op("pe", lambda e, b=b, h=h: e.transpose(out=ps[b][:, 0:128], in_=wsf[:, h, :], identity=ident[:]),
                 reads=[T_vg, tk("ident")], writes=[PS[b]])
            S.op("dve", lambda e, b=b, h=h: e.tensor_copy(out=wmtf[:, h * 128:(h + 1) * 128], in_=ps[b][:, 0:128]),
                 reads=[PS[b]], writes=T_t1)
            S.op("pool", lambda e, h=h: e.affine_select(out=wmtf[:, h * 128:(h + 1) * 128],
                                                        in_=wmtf[:, h * 128:(h + 1) * 128],
                                                        pattern=[[1, 128]], compare_op=ALU.is_ge, fill=0.0,
                                                        base=0, channel_multiplier=-1),
                 reads=T_t1, writes=T_t1)
            S.op("dve", lambda e, h=h: e.tensor_copy(out=wmtb[:, h, :], in_=wmtf[:, h * 128:(h + 1) * 128]),
                 reads=T_t1, writes=[tk("wmtb")])
            b2 = nb()
            S.op("pe", lambda e, b2=b2, h=h: e.matmul(ps[b2][:, 0:128], lhsT=ones_f[:], rhs=wmtf[:, h * 128:(h + 1) * 128],
                                                      start=True, stop=True),
                 reads=T_t1 + [tk("ones_f")], writes=[PS[b2]])
            S.op("dve", lambda e, b2=b2, h=h: e.scalar_tensor_tensor(
                out=bh[:, h, :], in0=ps[b2][:, 0:128], scalar=col(C_GVB + h), in1=bsb[:, h * 128:(h + 1) * 128],
                op0=ALU.mult, op1=ALU.add),
                reads=[PS[b2], tk("cv")] + T_ug, writes=[tk("bh")])

        def chunks_for(kind):
            seq = [("in", 6), ("in", 7)]
            if kind == "pre":
                return seq
            seq += [("in", 4), ("in", 5), ("in", 2), ("in", 3), ("in", 0), ("in", 1)]
            seq += [("out", c) for c in range(4)]
            if kind == "bnd":
                for g in range(3):
                    for q in range(4):
                        seq += [("up", g * 4 + q), ("up", 12 + g * 4 + q)]
                return seq

            def upg(g):
                r = []
                for q in range(4):
                    r += [("up", g * 4 + q), ("up", 12 + g * 4 + q)]
                return r

            def dng(g):
                return [("dn", g * 4 + c) for c in range(4)]
            seq += upg(0) + upg(1) + dng(0) + upg(2) + dng(1) + dng(2)
            return seq

        kinds = ["pre"] * npre + ["bnd"] + ["main"] * nmain
        wseq = []
        for kd in kinds:
            wseq += chunks_for(kd)
        scr = {"in": s_in, "out": s_out, "up": s_up, "dn": s_dn}
        wstate = {"issued": 0, "used": 0}
        WT = [tk("wr%d" % i) for i in range(NW)]

        def w_issue_upto(n):
            while wstate["issued"] < min(n, len(wseq)):
                i = wstate["issued"]
                kind, c = wseq[i]
                slot = i % NW
                src = scr[kind][c]
                S.dma("sp", lambda e, slot=slot, src=src: e.dma_start(out=wr[slot][:], in_=src), D_W[slot],
                      reads=[tk("s_%s%d" % (kind, c))], writes=[WT[slot]])
                wstate["issued"] += 1

        def next_w(expect, lookahead=True):
            i = wstate["used"]
            assert wseq[i] == expect, (wseq[i], expect)
            w_issue_upto(i + NW if lookahead else i + 1)
            wstate["used"] += 1
            slot = i % NW
            return wr[slot], WT[slot]

        def rstd_of(src_ap, src_tiles, n, scale, eps_col, dst_ap, dst_tile):
            S.op("act", lambda e: e.activation(out=dst_ap, in_=src_ap, func=AF.Sqrt, scale=scale,
                                               bias=epsc[:, eps_col:eps_col + 1]),
                 reads=src_tiles + [tk("epsc")], writes=[dst_tile])
            S.op("dve", lambda e: e.reciprocal(out=dst_ap, in_=dst_ap), reads=[dst_tile], writes=[dst_tile])

        XT = [tk("xres%d" % j) for j in range(NSUB)]
        HT = [tk("hT%d" % k) for k in range(16)]
        YT = [tk("yT%d" % k) for k in range(16)]

        def load_x(t):
            for j in range(NSUB):
                r0 = t * TT + j * 128
                S.dma("pool", lambda e, j=j, r0=r0: e.dma_start(out=xres[:, j, :], in_=xin[r0:r0 + 128, :]),
                      D_X[j], writes=[XT[j]])

        def norm_to_hT(gc0):
            T_ssq = tk("ssq")
            for j in range(NSUB):
                S.op("act", lambda e, j=j: e.activation(out=junk, in_=xres[:, j, :], func=AF.Square,
                                                        accum_out=sm[:, 32 + j:33 + j]),
                     reads=[XT[j]], writes=[T_vg, T_ssq])
            rstd_of(sm[:, 32:36], [T_ssq], 4, 1.0 / D, 0, sm[:, 36:40], tk("rstd"))
            for j in range(NSUB):
                S.op("dve", lambda e, j=j: e.tensor_scalar_mul(out=rb4[:, j, :], in0=ones_f[:], scalar1=sm[:, 36 + j:37 + j]),
                     reads=[tk("rstd"), tk("ones_f")], writes=[tk("rb4")])
            b = nb()
            for j in range(NSUB):
                S.op("pe", lambda e, j=j, b=b: e.transpose(out=ps[b][:, j * 128:(j + 1) * 128], in_=rb4[:, j, :], identity=ident[:]),
                     reads=[tk("rb4"), tk("ident")], writes=[PS[b]], inc=(j == NSUB - 1))
            S.op("act", lambda e, b=b: e.copy(out=rts[:], in_=ps[b][:]), reads=[PS[b]], writes=[tk("rts")])
            for k in range(16):
                b = nb()
                for j in range(NSUB):
                    S.op("pe", lambda e, j=j, b=b, k=k: e.transpose(out=ps[b][:, j * 128:(j + 1) * 128],
                                                                    in_=xres[:, j, k * 128:(k + 1) * 128], identity=ident[:]),
                         reads=[XT[j], tk("ident")], writes=[PS[b]], inc=(j == NSUB - 1))
                S.op("dve", lambda e, b=b, k=k: e.scalar_tensor_tensor(out=hT[:, k, :], in0=ps[b][:], scalar=col(gc0 + k),
                                                                       in1=rts[:], op0=ALU.mult, op1=ALU.mult),
                     reads=[PS[b], tk("cv"), tk("rts")], writes=[HT[k]])

        def mm_feat(W, WTk, m, b, rhs_of_k, rhs_tiles, nk=16):
            for k in range(nk):
                S.op("pe", lambda e, k=k, b=b, m=m: e.matmul(ps[b][:], lhsT=W[:, k, m * 128:(m + 1) * 128], rhs=rhs_of_k(k),
                                                              start=(k == 0), stop=(k == nk - 1)),
                     reads=[WTk, rhs_tiles[k]], writes=[PS[b]], inc=(k == nk - 1))

        TMP = [tk("tmp%d" % i) for i in range(8)]
        XL = [tk("xl%d" % h) for h in range(8)]
        GF = [tk("gf%d" % h) for h in range(8)]
        XR = [tk("xr0"), tk("xr1")]
        XRB = [tk("xrb0"), tk("xrb1")]
        YSQ = [tk("ysq%d" % h) for h in range(8)]

        def lru_conv(h):
            s = h % 2
            o = xr[:, s, :]
            S.op("dve", lambda e: e.tensor_scalar(out=o, in0=xl[:, h, 3:3 + TT], scalar1=col(C_LCW + 3 * 8 + h),
                                                  scalar2=col(C_LCB + h), op0=ALU.mult, op1=ALU.add),
                 reads=[XL[h], tk("cv")], writes=[XR[s]])
            for kk in (2, 1, 0):
                S.op("dve", lambda e, kk=kk: e.scalar_tensor_tensor(out=o, in0=xl[:, h, kk:kk + TT], scalar=col(C_LCW + kk * 8 + h),
                                                                    in1=o, op0=ALU.mult, op1=ALU.add),
                     reads=[XL[h], tk("cv"), XR[s]], writes=[XR[s]])
            S.op("act", lambda e: e.copy(out=xrb[:, s, :], in_=o), reads=[XR[s]], writes=[XRB[s]])
            S.op("pool", lambda e: e.tensor_copy(out=xl[:, h, 0:3], in_=xl[:, h, TT:TT + 3]), reads=[XL[h]], writes=[XL[h]])

        def lru_rest(h, full):
            s = h % 2
            base = (h % 2) * 4
            tr, ti, q, a = [tmp[:, base + i, 0:TT] for i in range(4)]
            Ttr, Tti, Tq, Ta = [TMP[base + i] for i in range(4)]
            b1 = nb()
            S.op("pe", lambda e: e.matmul(ps[b1][:], lhsT=wab[:, h, :], rhs=xrb[:, s, :], start=True, stop=True),
                 reads=[tk("wab"), XRB[s]], writes=[PS[b1]])
            b2 = nb()
            S.op("pe", lambda e: e.matmul(ps[b2][:], lhsT=wxb[:, h, :], rhs=xrb[:, s, :], start=True, stop=True),
                 reads=[tk("wxb"), XRB[s]], writes=[PS[b2]])
            S.op("act", lambda e: e.activation(out=tr, in_=ps[b1][:], func=AF.Tanh, scale=0.5, bias=dc[:, h:h + 1]),
                 reads=[PS[b1], tk("dc")], writes=[Ttr])
            S.op("act", lambda e: e.activation(out=ti, in_=ps[b2][:], func=AF.Tanh, scale=0.5, bias=dc[:, 8 + h:9 + h]),
                 reads=[PS[b2], tk("dc")], writes=[Tti])
            S.op("act", lambda e: e.activation(out=tr, in_=tr, func=AF.Tanh, scale=dc[:, 16 + h:17 + h], bias=dc[:, 16 + h:17 + h]),
                 reads=[Ttr, tk("dc")], writes=[Ttr])
            S.op("dve", lambda e: e.tensor_scalar_add(out=q, in0=tr, scalar1=1.0), reads=[Ttr], writes=[Tq])
            S.op("dve", lambda e: e.reciprocal(out=q, in_=q), reads=[Tq], writes=[Tq])
            S.op("dve", lambda e: e.tensor_scalar(out=a, in0=q, scalar1=2.0, scalar2=-1.0, op0=ALU.mult, op1=ALU.add),
                 reads=[Tq], writes=[Ta])
            S.op("act", lambda e: e.activation(out=tr, in_=tr, func=AF.Sqrt), reads=[Ttr], writes=[Ttr])
            S.op("dve", lambda e: e.tensor_mul(out=q, in0=q, in1=tr), reads=[Tq, Ttr], writes=[Tq])
            S.op("dve", lambda e: e.scalar_tensor_tensor(out=ti, in0=ti, scalar=1.0, in1=xr[:, s, :], op0=ALU.add, op1=ALU.mult),
                 reads=[Tti, XR[s]], writes=[Tti])
            S.op("dve", lambda e: e.tensor_mul(out=q, in0=q, in1=ti), reads=[Tq, Tti], writes=[Tq])
            S.op("dve", lambda e: e.tensor_tensor_scan(out=ti, data0=a, data1=q, initial=state[:, h:h + 1],
                                                       op0=ALU.mult, op1=ALU.add),
                 reads=[Ta, Tq, tk("state")], writes=[Tti])
            S.op("pool", lambda e: e.tensor_copy(out=state[:, h:h + 1], in_=ti[:, TT - 1:TT]), reads=[Tti], writes=[tk("state")])
            if full:
                S.op("dve", lambda e: e.tensor_mul(out=tr, in0=ti, in1=gf[:, h, :]), reads=[Tti, GF[h]], writes=[Ttr])
                S.op("act", lambda e: e.activation(out=ysq[:, h, :], in_=tr, func=AF.Square), reads=[Ttr], writes=[YSQ[h]])
                S.op("act", lambda e: e.activation(out=yT[:, 8 + h, :], in_=tr, func=AF.Copy, scale=col(C_LRO + h)),
                     reads=[Ttr, tk("cv")], writes=[YT[8 + h]])

        def ssq_group(cbase, dst_cols):
            b = nb()
            for j in range(NSUB):
                for h in range(8):
                    S.op("pe", lambda e, j=j, h=h, b=b: e.matmul(ps[b][:, j:j + 1], lhsT=ysq[:, h, j * 128:(j + 1) * 128],
                                                                  rhs=ones_b[:, 0:1], start=(h == 0), stop=(h == 7)),
                         reads=[YSQ[h], tk("ones_b")], writes=[PS[b]], inc=(j == NSUB - 1 and h == 7))
            rstd_of(ps[b][:, 0:4], [PS[b]], 4, 1.0 / 1024, 0, sm[:, dst_cols:dst_cols + 4], tk("rs%d" % dst_cols))

        def mixer(kind):
            full = kind != "pre"
            for c in (6, 7):
                W, WTk = next_w(("in", c))
                for m in range(4):
                    h = (c - 6) * 4 + m
                    b = nb()
                    mm_feat(W, WTk, m, b, lambda k: hT[:, k, :], HT)
                    S.op("act", lambda e, b=b, h=h: e.copy(out=xl[:, h, 3:3 + TT], in_=ps[b][:]), reads=[PS[b]], writes=[XL[h]])
            if not full:
                lru_conv(0)
                for h in range(8):
                    if h + 1 < 8:
                        lru_conv(h + 1)
                    lru_rest(h, False)
                return
            for c in (4, 5):
                W, WTk = next_w(("in", c))
                for m in range(4):
                    h = (c - 4) * 4 + m
                    b = nb()
                    mm_feat(W, WTk, m, b, lambda k: hT[:, k, :], HT)
                    S.op("act", lambda e, b=b, h=h: e.activation(out=gf[:, h, :], in_=ps[b][:], func=AF.Gelu_apprx_tanh),
                         reads=[PS[b]], writes=[GF[h]])
            Wv0, WT0 = next_w(("in", 2))
            Wv1, WT1 = next_w(("in", 3), lookahead=False)
            lru_conv(0)
            hh = 0
            for j in range(NSUB):
                for _ in range(2):
                    if hh + 1 < 8:
                        lru_conv(hh + 1)
                    lru_rest(hh, True)
                    hh += 1
                for half, (W, WTk) in enumerate(((Wv0, WT0), (Wv1, WT1))):
                    b = nb()
                    for k in range(16):
                        S.op("pe", lambda e, k=k, b=b, j=j, W=W: e.matmul(ps[b][:], lhsT=hT[:, k, j * 128:(j + 1) * 128], rhs=W[:, k, :],
                                                                           start=(k == 0), stop=(k == 15)),
                             reads=[WTk, HT[k]], writes=[PS[b]], inc=(k == 15))
                    S.op("act", lambda e, b=b, half=half: e.activation(out=vg[:, half * 512:(half + 1) * 512], in_=ps[b][:],
                                                                       func=AF.Gelu_apprx_tanh, accum_out=sm[:, 40 + half:41 + half]),
                         reads=[PS[b]], writes=[T_vg, tk("lns")])
                S.op("act", lambda e: e.activation(out=t1[:].rearrange("p a b -> p (a b)"), in_=vg[:], func=AF.Square,
                                                   accum_out=sm[:, 42:43]),
                     reads=[T_vg], writes=T_t1 + [tk("lns")])
                Tl = tk("lns")
                S.op("dve", lambda e: e.tensor_add(out=sm[:, 43:44], in0=sm[:, 40:41], in1=sm[:, 41:42]), reads=[Tl], writes=[Tl])
                S.op("dve", lambda e: e.tensor_scalar_mul(out=sm[:, 43:44], in0=sm[:, 43:44], scalar1=1.0 / 1024), reads=[Tl], writes=[Tl])
                S.op("dve", lambda e: e.tensor_mul(out=sm[:, 44:45], in0=sm[:, 43:44], in1=sm[:, 43:44]), reads=[Tl], writes=[Tl])
                S.op("dve", lambda e: e.scalar_tensor_tensor(out=sm[:, 45:46], in0=sm[:, 42:43], scalar=1.0 / 1024, in1=sm[:, 44:45],
                                                             op0=ALU.mult, op1=ALU.subtract), reads=[Tl], writes=[Tl])
                rstd_of(sm[:, 45:46], [Tl], 1, 1.0, 1, sm[:, 46:47], Tl)
                S.op("dve", lambda e: e.scalar_tensor_tensor(out=sm[:, 47:48], in0=sm[:, 43:44], scalar=-1.0, in1=sm[:, 46:47],
                                                             op0=ALU.mult, op1=ALU.mult), reads=[Tl], writes=[Tl])
                S.op("dve", lambda e, j=j: e.tensor_scalar(out=vn[:, j, :], in0=vg[:], scalar1=sm[:, 46:47], scalar2=sm[:, 47:48],
                                                           op0=ALU.mult, op1=ALU.add),
                     reads=[T_vg, Tl], writes=[tk("vn%d" % j)])
            w_issue_upto(wstate["used"] + NW)
            ssq_group(0, 52)
            for c in (0, 1):
                W, WTk = next_w(("in", c))
                for m in range(4):
                    h = c * 4 + m
                    s = h % 2
                    b = nb()
                    mm_feat(W, WTk, m, b, lambda k: hT[:, k, :], HT)
                    S.op("act", lambda e, b=b, s=s: e.activation(out=ug[:, s, :], in_=ps[b][:], func=AF.Gelu_apprx_tanh),
                         reads=[PS[b]], writes=[T_ug[s]])
                    b2 = nb()
                    for j in range(NSUB):
                        S.op("pe", lambda e, j=j, h=h, b2=b2: e.matmul(ps[b2][:, j * 128:(j + 1) * 128], lhsT=vn[:, j, h * 128:(h + 1) * 128],
                                                                        rhs=wmtb[:, h, :], start=True, stop=True),
                             reads=[tk("vn%d" % j), tk("wmtb")], writes=[PS[b2]], inc=(j == NSUB - 1))
                    for j in range(NSUB):
                        S.op("dve", lambda e, j=j, h=h, b2=b2, s=s: e.scalar_tensor_tensor(
                            out=t1[:, s, j * 128:(j + 1) * 128], in0=ps[b2][:, j * 128:(j + 1) * 128], scalar=col(C_GVG + h),
                            in1=bh[:, h, :], op0=ALU.mult, op1=ALU.add),
                            reads=[PS[b2], tk("cv"), tk("bh")], writes=[T_t1[s]])
                    S.op("dve", lambda e, s=s: e.tensor_mul(out=t1[:, s, :], in0=t1[:, s, :], in1=ug[:, s, :]),
                         reads=[T_t1[s], T_ug[s]], writes=[T_t1[s]])
                    S.op("act", lambda e, s=s, h=h: e.activation(out=ysq[:, h, :], in_=t1[:, s, :], func=AF.Square),
                         reads=[T_t1[s]], writes=[YSQ[h]])
                    S.op("act", lambda e, s=s, h=h: e.activation(out=yT[:, h, :], in_=t1[:, s, :], func=AF.Copy, scale=col(C_GMO + h)),
                         reads=[T_t1[s], tk("cv")], writes=[YT[h]])
            ssq_group(0, 48)
            for c in range(4):
                W, WTk = next_w(("out", c))
                for j in range(NSUB):
                    for grp, rc in ((0, 48), (1, 52)):
                        b = nb()
                        for kk in range(8):
                            k = grp * 8 + kk
                            S.op("pe", lambda e, k=k, kk=kk, b=b, j=j, W=W: e.matmul(ps[b][:], lhsT=yT[:, k, j * 128:(j + 1) * 128], rhs=W[:, k, :],
                                                                                      start=(kk == 0), stop=(kk == 7)),
                                 reads=[WTk, YT[k]], writes=[PS[b]], inc=(kk == 7))
                        S.op("dve", lambda e, b=b, j=j, c=c, rc=rc: e.scalar_tensor_tensor(
                            out=xres[:, j, c * 512:(c + 1) * 512], in0=ps[b][:], scalar=sm[:, rc + j:rc + j + 1],
                            in1=xres[:, j, c * 512:(c + 1) * 512], op0=ALU.mult, op1=ALU.add),
                            reads=[PS[b], tk("rs%d" % rc), XT[j]], writes=[XT[j]])

        UH = tk("uh")

        def ffn_up_group(g, hbuf, halo_only):
            for q in range(4):
                Wg, WTg = next_w(("up", g * 4 + q))
                Wv, WTv = next_w(("up", 12 + g * 4 + q), lookahead=False)
                for m in range(4):
                    kh = q * 4 + m
                    fg = (g * 4 + q) * 4 + m
                    fv = 48 + fg
                    for which, (W, WTk, f) in enumerate(((Wg, WTg, fg), (Wv, WTv, fv))):
                        b = nb()
                        if halo_only:
                            for k in range(16):
                                S.op("pe", lambda e, k=k, b=b, m=m, W=W: e.matmul(ps[b][:, 0:2], lhsT=W[:, k, m * 128:(m + 1) * 128],
                                                                                   rhs=hT[:, k, TT - 2:TT], start=(k == 0), stop=(k == 15)),
                                     reads=[WTk, HT[k]], writes=[PS[b]], inc=(k == 15))
                            S.op("act", lambda e, b=b, f=f: e.copy(out=uh[:, f, :], in_=ps[b][:, 0:2]), reads=[PS[b]], writes=[UH])
                            continue
                        mm_feat(W, WTk, m, b, lambda k: hT[:, k, :], HT)
                        upb = tmp[:, which, :]
                        Tup = TMP[which]
                        acc = tmp[:, 2 + which, 0:TT]
                        Tacc = TMP[2 + which]
                        S.op("pool", lambda e, upb=upb, f=f: e.tensor_copy(out=upb[:, 0:2], in_=uh[:, f, :]), reads=[UH], writes=[Tup])
                        S.op("act", lambda e, upb=upb, b=b: e.copy(out=upb[:, 2:2 + TT], in_=ps[b][:]), reads=[PS[b]], writes=[Tup])
                        S.op("pool", lambda e, upb=upb, f=f: e.tensor_copy(out=uh[:, f, :], in_=upb[:, TT:TT + 2]), reads=[Tup], writes=[UH])
                        S.op("dve", lambda e, upb=upb, acc=acc, f=f: e.tensor_scalar_mul(out=acc, in0=upb[:, 2:2 + TT], scalar1=col(C_FCW + 2 * 96 + f)),
                             reads=[Tup, tk("cv")], writes=[Tacc])
                        for kk in (1, 0):
                            S.op("dve", lambda e, upb=upb, acc=acc, f=f, kk=kk: e.scalar_tensor_tensor(
                                out=acc, in0=upb[:, kk:kk + TT], scalar=col(C_FCW + kk * 96 + f), in1=acc, op0=ALU.mult, op1=ALU.add),
                                reads=[Tup, tk("cv"), Tacc], writes=[Tacc])
                        if which == 0:
                            gg = tmp[:, 4, 0:TT]
                            S.op("act", lambda e, acc=acc, f=f, gg=gg: e.activation(out=gg, in_=acc, func=AF.Gelu_apprx_tanh, bias=col(C_FCB + f)),
                                 reads=[Tacc, tk("cv")], writes=[TMP[4]])
                        else:
                            gg = tmp[:, 4, 0:TT]
                            S.op("dve", lambda e, acc=acc, f=f, gg=gg, kh=kh: e.scalar_tensor_tensor(
                                out=hid_ap(hbuf, kh), in0=acc, scalar=col(C_FCB + f), in1=gg, op0=ALU.add, op1=ALU.mult),
                                reads=[Tacc, tk("cv"), TMP[4]], writes=[hid_tk(hbuf, kh)])
                w_issue_upto(wstate["used"] + NW)

        def ffn_down_group(g, hbuf):
            for c in range(4):
                W, WTk = next_w(("dn", g * 4 + c))
                for j in range(NSUB):
                    b = nb()
                    for k in range(16):
                        S.op("pe", lambda e, k=k, b=b, j=j, W=W: e.matmul(ps[b][:], lhsT=hid_ap(hbuf, k)[:, j * 128:(j + 1) * 128], rhs=W[:, k, :],
                                                                           start=(k == 0), stop=(k == 15)),
                             reads=[WTk, hid_tk(hbuf, k)], writes=[PS[b]], inc=(k == 15))
                    S.op("dve", lambda e, b=b, j=j, c=c: e.tensor_add(out=xres[:, j, c * 512:(c + 1) * 512], in0=ps[b][:],
                                                                      in1=xres[:, j, c * 512:(c + 1) * 512]),
                         reads=[PS[b], XT[j]], writes=[XT[j]])

        def final_store(ot):
            T_fs = tk("fssq")
            for j in range(NSUB):
                S.op("act", lambda e, j=j: e.activation(out=junk, in_=xres[:, j, :], func=AF.Square, accum_out=sm[:, 56 + j:57 + j]),
                     reads=[XT[j]], writes=[T_vg, T_fs])
            rstd_of(sm[:, 56:60], [T_fs], 4, 1.0 / D, 0, sm[:, 60:64], tk("frstd"))
            for j in range(NSUB):
                S.op("dve", lambda e, j=j: e.scalar_tensor_tensor(out=xres[:, j, :], in0=xres[:, j, :], scalar=sm[:, 60 + j:61 + j],
                                                                  in1=gfb[:], op0=ALU.mult, op1=ALU.mult),
                     reads=[XT[j], tk("frstd"), tk("gfb")], writes=[XT[j]])
                r0 = ot * TT + j * 128
                S.dma("pool", lambda e, j=j, r0=r0: e.dma_start(out=out[r0:r0 + 128, :], in_=xres[:, j, :]), D_O[j], reads=[XT[j]])

        for e_ in ("pe", "act", "dve", "pool"):
            S.wait_all(e_, cst_sems)

        for t, kind in enumerate(kinds):
            load_x(t)
            left = len(cvt_jobs) - cvt_pos[0]
            if left > 0:
                if kind == "pre":
                    emit_cvts(-(-left // (npre - t)))
                else:
                    emit_cvts(left)
            norm_to_hT(C_G1)
            mixer(kind)
            if kind == "pre":
                continue
            if kind == "bnd":
                norm_to_hT(C_G2)
                for g in range(3):
                    ffn_up_group(g, 0, True)
                S.op("dve", lambda e: e.tensor_scalar_mul(out=state[:], in0=state[:], scalar1=flag[:, 0:1]),
                     reads=[tk("state"), tk("flag")], writes=[tk("state")])
                S.op("dve", lambda e: e.tensor_scalar_mul(out=uh[:].rearrange("p a b -> p (a b)"), in0=uh[:].rearrange("p a b -> p (a b)"),
                                                          scalar1=flag[:, 0:1]),
                     reads=[UH, tk("flag")], writes=[UH])
                for h in range(8):
                    S.op("dve", lambda e, h=h: e.tensor_scalar_mul(out=xl[:, h, 0:3], in0=xl[:, h, 0:3], scalar1=flag[:, 0:1]),
                         reads=[XL[h], tk("flag")], writes=[XL[h]])
                continue
            dbg = debug and t == nmain
            if dbg:
                D_DBG = S.dsem("dbg")
                for j in range(NSUB):
                    S.dma("pool", lambda e, j=j: e.dma_start(out=dbg_xmid[j * 128:(j + 1) * 128, :], in_=xres[:, j, :]), D_DBG, reads=[XT[j]])
                S.dma("pool", lambda e: e.dma_start(out=dbg_yT, in_=yT[:]), D_DBG, reads=YT)
            norm_to_hT(C_G2)
            if dbg:
                S.dma("pool", lambda e: e.dma_start(out=dbg_hT, in_=hT[:]), D_DBG, reads=HT)
            ffn_up_group(0, 0, False)
            if dbg:
                S.dma("pool", lambda e: e.dma_start(out=dbg_hid, in_=hid0.rearrange("p (k t) -> p k t", k=16)), D_DBG, reads=GF)
            ffn_up_group(1, 1, False)
            ffn_down_group(0, 0)
            ffn_up_group(2, 0, False)
            ffn_down_group(1, 1)
            ffn_down_group(2, 0)
            if dbg:
                for j in range(NSUB):
                    S.dma("pool", lambda e, j=j: e.dma_start(out=dbg_xffn[j * 128:(j + 1) * 128, :], in_=xres[:, j, :]), D_DBG, reads=[XT[j]])
                S.wait_all("pool", [D_DBG])
            final_store(t - nmain)
        assert wstate["used"] == len(wseq)
        S.wait_all("pool", D_O)
        S.wait_all("sp", D_W)

        with nc.Block() as block:
            @block.tensor
            def _(e):
                S.emit("pe", e)

            @block.scalar
            def _(e):
                S.emit("act", e)

            @block.vector
            def _(e):
                S.emit("dve", e)

            @block.gpsimd
            def _(e):
                S.emit("pool", e)

            @block.sync
            def _(e):
                S.emit("sp", e)
    return nc


_W_NAMES = ["norm1_g", "w_in", "gm_v_g", "gm_v_b", "gm_ws", "gm_bs", "lru_conv_w", "lru_conv_b",
            "lru_wa", "lru_ba", "lru_wx", "lru_bx", "lru_lambda", "gm_out_g", "lru_out_g", "w_out",
            "norm2_g", "ffn_w_up", "ffn_conv_w", "ffn_conv_b", "ffn_w_down", "final_g"]


def run(inputs, nmain, n_cores, trace=False, debug=False):
    x = np.asarray(inputs["x"], dtype=np.float32)
    half = nmain * TT
    ws = {k: np.ascontiguousarray(np.asarray(inputs[k], dtype=np.float32)) for k in _W_NAMES}
    in_maps = []
    for c in range(n_cores):
        b, hf = c // 2, c % 2
        own = x[b, hf * half:(hf + 1) * half]
        prev = x[b, 0:half] if hf else np.zeros_like(own)
        m = dict(ws)
        m["xin"] = np.concatenate([prev, own], axis=0)
        m["flag"] = np.full((128, 1), float(hf), np.float32)
        in_maps.append(m)
    nc = build(nmain, debug)
    res = run_bass_kernel_spmd(nc, in_maps, core_ids=list(range(n_cores)), trace=trace)
    nb_ = n_cores // 2
    o = np.empty((nb_, 2 * half, D), np.float32)
    for c in range(n_cores):
        b, hf = c // 2, c % 2
        o[b, hf * half:(hf + 1) * half] = res.results[c]["out"]
    return o, res


def kernel(**inputs):
    o, _ = run(inputs, SEQ // 2 // TT, N_CORES)
    return o
```
